# Optimizing a Trainium2 kernel written in Bass

```python
import math
import jax
import jax.numpy as jnp
from jax import lax
import numpy as np

D_MODEL = 2048
BATCH = 4
SEQ = 4096
DEPTH = 4

GRID_W = 64
CTX_LEN = 256
N_BRANCH = 4
BRANCH_W = D_MODEL // 4
FN_GROUPS = 4
FN_GW = BRANCH_W // FN_GROUPS
DN_HEADS = 4
DN_DK = BRANCH_W // DN_HEADS
DN_DV = BRANCH_W // DN_HEADS
DN_CONV = 3
DN_CHUNK = 64
HG_HEADS = 4
HG_DK = BRANCH_W // HG_HEADS
HG_DV = BRANCH_W // HG_HEADS
HG_CHUNK = 64
DA_HEADS = 4
DA_DH = BRANCH_W // (2 * DA_HEADS)
DA_DV = 2 * DA_DH
Q_BLOCK = 128
ROPE_THETA = 10000.0
EPS = 1e-6

IN_LAYOUT = (
    ('fn_u', BRANCH_W), ('fn_z', BRANCH_W),
    ('dn_q', BRANCH_W), ('dn_k', BRANCH_W), ('dn_v', BRANCH_W), ('dn_z', BRANCH_W),
    ('dn_a', 2 * DN_HEADS), ('dn_b', 2 * DN_HEADS),
    ('hg_q', BRANCH_W), ('hg_f', 2 * BRANCH_W), ('hg_i', BRANCH_W), ('hg_z', BRANCH_W),
    ('da_q', BRANCH_W), ('da_k', BRANCH_W), ('da_v', BRANCH_W), ('da_z', BRANCH_W),
    ('gate', N_BRANCH * D_MODEL),
)
IN_W = sum(w for _, w in IN_LAYOUT)

kernel_name = 'hybrid_parallel_gated_diffusion_trunk'


def _rms(x, g):
    xf = x.astype(jnp.float32)
    y = xf * lax.rsqrt(jnp.mean(xf * xf, axis=-1, keepdims=True) + EPS)
    return (y * g.astype(jnp.float32)).astype(x.dtype)


def _head_rms(o, g, dtype):
    y = o * lax.rsqrt(jnp.mean(o * o, axis=-1, keepdims=True) + EPS) * g.astype(jnp.float32)
    return y.reshape(o.shape[:2] + (-1,)).astype(dtype)


def _l2n(x):
    return x * lax.rsqrt(jnp.sum(x * x, axis=-1, keepdims=True) + EPS)


def _split_proj(p):
    out, off = {}, 0
    for name, w in IN_LAYOUT:
        out[name] = p[..., off:off + w]
        off += w
    return out


def _stream_in(x, mod, norm_g, w_in):
    shift, scale, gate = jnp.split(mod, 3, axis=-1)
    h = _rms(x, norm_g) * (1.0 + scale) + shift
    return _split_proj(h @ w_in), gate


def _to_chunks(a, c):
    b, t = a.shape[:2]
    a = a.reshape((b, t // c, c) + a.shape[2:])
    return jnp.moveaxis(a, (1, 3), (0, 2))


def _from_chunks(a):
    a = jnp.moveaxis(a, (0, 2), (1, 3))
    return a.reshape((a.shape[0], -1) + a.shape[3:])


def _fourier(u, w, b):
    bsz, t, _ = u.shape
    ug = u.astype(jnp.float32).reshape(bsz, t, FN_GROUPS, FN_GW)
    f = jnp.fft.fft2(ug, axes=(1, 3), norm='ortho').real
    return f.reshape(bsz, t, BRANCH_W).astype(u.dtype) @ w + b


def _short_conv(u, w):
    pad = DN_CONV // 2
    t = u.shape[1]
    up = jnp.pad(u, ((0, 0), (pad, pad), (0, 0)))
    y = sum(up[:, k:k + t] * w[k] for k in range(DN_CONV))
    return jax.nn.silu(y)


def _dn_prep(p, conv_w, a_log, dt_bias):
    bsz, t, _ = p['dn_q'].shape
    qkv = _short_conv(jnp.concatenate([p['dn_q'], p['dn_k'], p['dn_v']], axis=-1), conv_w)
    q, k, v = jnp.split(qkv.astype(jnp.float32), 3, axis=-1)
    q = _l2n(q.reshape(bsz, t, DN_HEADS, DN_DK)) * DN_DK ** -0.5
    k = _l2n(k.reshape(bsz, t, DN_HEADS, DN_DK))
    v = v.reshape(bsz, t, DN_HEADS, DN_DV)
    a = p['dn_a'].astype(jnp.float32).reshape(bsz, t, 2, DN_HEADS)
    g = -jnp.exp(a_log.astype(jnp.float32)) * jax.nn.softplus(a + dt_bias.astype(jnp.float32))
    beta = jax.nn.sigmoid(p['dn_b'].astype(jnp.float32).reshape(bsz, t, 2, DN_HEADS))
    return q, k, v, g, beta


def _delta_chunked(q, k, v, g, beta, s0):
    c = DN_CHUNK
    q, k, v = _to_chunks(q, c), _to_chunks(k, c), _to_chunks(v, c)
    g, beta = _to_chunks(g, c), _to_chunks(beta, c)
    gc = jnp.cumsum(g, axis=-1)
    idx = jnp.arange(c)
    incl = idx[:, None] >= idx[None, :]
    strict = idx[:, None] > idx[None, :]
    decay = jnp.exp(jnp.where(incl, gc[..., :, None] - gc[..., None, :], -jnp.inf))
    kb = k * beta[..., None]
    a_kk = jnp.where(strict, jnp.einsum('nbhid,nbhjd->nbhij', kb, k) * decay, 0.0)
    eye = jnp.eye(c, dtype=a_kk.dtype)
    t_inv = lax.linalg.triangular_solve(a_kk + eye, jnp.broadcast_to(eye, a_kk.shape),
                                        left_side=True, lower=True, unit_diagonal=True)
    u = t_inv @ (v * beta[..., None])
    w = t_inv @ (kb * jnp.exp(gc)[..., None])
    a_qk = jnp.where(incl, jnp.einsum('nbhid,nbhjd->nbhij', q, k) * decay, 0.0)
    q_dec = q * jnp.exp(gc)[..., None]
    k_dec = k * jnp.exp(gc[..., -1:] - gc)[..., None]
    g_last = jnp.exp(gc[..., -1])[..., None, None]

    def step(s, xs):
        u_n, w_n, aqk_n, qd_n, kd_n, gl_n = xs
        v_new = u_n - w_n @ s
        o_n = qd_n @ s + aqk_n @ v_new
        s = s * gl_n + jnp.einsum('bhcd,bhce->bhde', kd_n, v_new)
        return s, o_n

    s, o = lax.scan(step, s0, (u, w, a_qk, q_dec, k_dec, g_last))
    return _from_chunks(o), s


def _dn_bidir(q, k, v, g, beta, s_f, s_b):
    flip = lambda a: jnp.flip(a, axis=1)
    o_f, s_f = _delta_chunked(q, k, v, g[:, :, 0], beta[:, :, 0], s_f)
    o_b, s_b = _delta_chunked(flip(q), flip(k), flip(v), flip(g[:, :, 1]), flip(beta[:, :, 1]), s_b)
    return o_f + flip(o_b), s_f, s_b


def _hg_prep(p, lb):
    bsz, t, _ = p['hg_q'].shape
    q = jax.nn.silu(p['hg_q'].astype(jnp.float32)).reshape(bsz, t, HG_HEADS, HG_DK)
    f = p['hg_f'].astype(jnp.float32).reshape(bsz, t, 2, HG_HEADS * HG_DK)
    log_f = jnp.logaddexp(jnp.log(lb), jnp.log1p(-lb) + jax.nn.log_sigmoid(f))
    k = (1.0 - lb) * jax.nn.sigmoid(-f)
    v = p['hg_i'].astype(jnp.float32).reshape(bsz, t, HG_HEADS, HG_DV)
    return (q, k.reshape(bsz, t, 2, HG_HEADS, HG_DK), v,
            log_f.reshape(bsz, t, 2, HG_HEADS, HG_DK))


def _hgrn2_chunked(q, k, v, log_f, s0):
    c = HG_CHUNK
    q, k, v, log_f = (_to_chunks(a, c) for a in (q, k, v, log_f))
    gc = jnp.cumsum(log_f, axis=-2)
    idx = jnp.arange(c)
    incl = idx[:, None] >= idx[None, :]

    def step(s, xs):
        q_n, k_n, v_n, g_n = xs
        diff = g_n[..., :, None, :] - g_n[..., None, :, :]
        dec = jnp.exp(jnp.where(incl[:, :, None], diff, -jnp.inf))
        a_qk = jnp.einsum('bhid,bhjd,bhijd->bhij', q_n, k_n, dec)
        o_n = (q_n * jnp.exp(g_n)) @ s + a_qk @ v_n
        g_last = g_n[..., -1:, :]
        s = (jnp.exp(g_last[..., 0, :])[..., None] * s
             + jnp.einsum('bhcd,bhce->bhde', k_n * jnp.exp(g_last - g_n), v_n))
        return s, o_n

    s, o = lax.scan(step, s0, (q, k, v, gc))
    return _from_chunks(o), s


def _hg_bidir(q, k, v, log_f, s_f, s_b):
    flip = lambda a: jnp.flip(a, axis=1)
    o_f, s_f = _hgrn2_chunked(q, k[:, :, 0], v, log_f[:, :, 0], s_f)
    o_b, s_b = _hgrn2_chunked(flip(q), flip(k[:, :, 1]), flip(v), flip(log_f[:, :, 1]), s_b)
    return o_f + flip(o_b), s_f, s_b


def _axial_angles(t):
    rows = t // GRID_W
    pos = jnp.arange(rows * GRID_W)
    row = (pos // GRID_W).astype(jnp.float32)
    col = (pos % GRID_W).astype(jnp.float32)
    n = DA_DH // 4
    inv = ROPE_THETA ** (-jnp.arange(n, dtype=jnp.float32) / n)
    return row[:, None] * inv, col[:, None] * inv


def _rope_half(x, ang):
    x1, x2 = jnp.split(x, 2, axis=-1)
    cos = jnp.cos(ang)[None, :, None, None, :].astype(x.dtype)
    sin = jnp.sin(ang)[None, :, None, None, :].astype(x.dtype)
    return jnp.concatenate([x1 * cos - x2 * sin, x2 * cos + x1 * sin], axis=-1)


def _axial_rope(x, ang_r, ang_c):
    half = DA_DH // 2
    return jnp.concatenate([_rope_half(x[..., :half], ang_r), _rope_half(x[..., half:], ang_c)], axis=-1)


def _da_prep(p):
    bsz, t, _ = p['da_q'].shape
    q = p['da_q'].reshape(bsz, t, DA_HEADS, 2, DA_DH)
    k = p['da_k'].reshape(bsz, t, DA_HEADS, 2, DA_DH)
    v = p['da_v'].reshape(bsz, t, DA_HEADS, DA_DV)
    return q, k, v


def _diff_block(qb, k, v, lam):
    s = jnp.einsum('bqhmd,bkhmd->bhmqk', qb, k).astype(jnp.float32) * DA_DH ** -0.5
    p = jax.nn.softmax(s, axis=-1)
    a = p[:, :, 0] - lam * p[:, :, 1]
    return jnp.einsum('bhqk,bkhd->bqhd', a.astype(v.dtype), v)


def _diff_latent(q, k_all, v_all, lam):
    bsz, t = q.shape[:2]
    qb = jnp.moveaxis(q.reshape((bsz, t // Q_BLOCK, Q_BLOCK) + q.shape[2:]), 1, 0)
    o = lax.map(lambda blk: _diff_block(blk, k_all, v_all, lam), qb)
    return jnp.moveaxis(o, 0, 1).reshape(bsz, t, DA_HEADS, DA_DV)


def _branches(p, o_dn, o_hg, o_da, fn_w, fn_b, dn_norm, hg_norm, da_norm, lam_init):
    dt = p['fn_u'].dtype
    y_fn = _fourier(p['fn_u'], fn_w, fn_b) * jax.nn.silu(p['fn_z'])
    y_dn = _head_rms(o_dn, dn_norm, dt) * jax.nn.silu(p['dn_z'])
    y_hg = _head_rms(o_hg, hg_norm, dt) * jax.nn.silu(p['hg_z'])
    y_da = _head_rms(o_da.astype(jnp.float32), da_norm, dt) * (1.0 - lam_init) * jax.nn.silu(p['da_z'])
    return (y_fn, y_dn, y_hg, y_da)


def _merge(gate_logits, ys, w_branch, w_out):
    bsz, t, _ = gate_logits.shape
    g = jax.nn.sigmoid(gate_logits).reshape(bsz, t, N_BRANCH, D_MODEL)
    y = sum(g[:, :, n] * (ys[n] @ w_branch[n]) for n in range(N_BRANCH))
    return y @ w_out


def setup_inputs(seed: int = 0) -> dict:
    key = jax.random.key(seed)
    ks = jax.random.split(key, 24)
    f32 = jnp.float32
    nrm = lambda k, shape, scale: jax.random.normal(k, shape, f32) * scale
    x = nrm(ks[0], (BATCH, SEQ, D_MODEL), 1.0)
    c = nrm(ks[1], (BATCH, D_MODEL), 1.0)
    ctx = nrm(ks[2], (BATCH, CTX_LEN, D_MODEL), 1.0)
    c_ctx = nrm(ks[3], (D_MODEL,), 1.0)
    norm_g = 1.0 + nrm(ks[4], (DEPTH, D_MODEL), 0.02)
    w_ada = nrm(ks[5], (DEPTH, D_MODEL, 3 * D_MODEL), 0.5 * D_MODEL ** -0.5)
    b_ada = nrm(ks[6], (DEPTH, 3 * D_MODEL), 0.02)
    w_in = nrm(ks[7], (DEPTH, D_MODEL, IN_W), D_MODEL ** -0.5)
    fn_w = nrm(ks[8], (DEPTH, BRANCH_W, BRANCH_W), BRANCH_W ** -0.5)
    fn_b = nrm(ks[9], (DEPTH, BRANCH_W), 0.02)
    dn_conv = nrm(ks[10], (DEPTH, DN_CONV, 3 * BRANCH_W), DN_CONV ** -0.5)
    dn_a_log = jnp.log(jax.random.uniform(ks[11], (DEPTH, 2, DN_HEADS), f32, 1.0, 16.0))
    dt0 = jnp.exp(jax.random.uniform(ks[12], (DEPTH, 2, DN_HEADS), f32, math.log(1e-3), math.log(1e-1)))
    dn_dt_bias = dt0 + jnp.log(-jnp.expm1(-dt0))
    dn_norm = 1.0 + nrm(ks[13], (DEPTH, DN_DV), 0.02)
    hg_lb_logits = nrm(ks[14], (2, DEPTH, HG_HEADS * HG_DK), 0.5)
    hg_norm = 1.0 + nrm(ks[15], (DEPTH, HG_DV), 0.02)
    da_lambda = nrm(ks[16], (DEPTH, 4, DA_DH), 0.1)
    da_norm = 1.0 + nrm(ks[17], (DEPTH, DA_DV), 0.02)
    w_branch = nrm(ks[18], (DEPTH, N_BRANCH, BRANCH_W, D_MODEL), BRANCH_W ** -0.5)
    w_out = nrm(ks[19], (DEPTH, D_MODEL, D_MODEL), D_MODEL ** -0.5)
    final_g = 1.0 + nrm(ks[20], (D_MODEL,), 0.02)
    return {'x': x, 'c': c, 'ctx': ctx, 'c_ctx': c_ctx, 'norm_g': norm_g, 'w_ada': w_ada,
            'b_ada': b_ada, 'w_in': w_in, 'fn_w': fn_w, 'fn_b': fn_b, 'dn_conv': dn_conv,
            'dn_a_log': dn_a_log, 'dn_dt_bias': dn_dt_bias, 'dn_norm': dn_norm,
            'hg_lb_logits': hg_lb_logits, 'hg_norm': hg_norm, 'da_lambda': da_lambda,
            'da_norm': da_norm, 'w_branch': w_branch, 'w_out': w_out, 'final_g': final_g}


def reference(x, c, ctx, c_ctx, norm_g, w_ada, b_ada, w_in, fn_w, fn_b, dn_conv, dn_a_log,
              dn_dt_bias, dn_norm, hg_lb_logits, hg_norm, da_lambda, da_norm, w_branch, w_out,
              final_g):
    bsz, t, _ = x.shape
    ang_r, ang_c = _axial_angles(t)
    lb_all = jnp.cumsum(jax.nn.softmax(hg_lb_logits.astype(jnp.float32), axis=1), axis=1)
    lb_all = lb_all - lb_all[:, :1]
    silu_c = jax.nn.silu(c)
    silu_cc = jax.nn.silu(c_ctx)
    xl, xc = x, ctx
    for l in range(DEPTH):
        last = l == DEPTH - 1
        mod_l = (silu_c @ w_ada[l] + b_ada[l])[:, None, :]
        mod_c = silu_cc @ w_ada[l] + b_ada[l]
        pl, gate_l = _stream_in(xl, mod_l, norm_g[l], w_in[l])
        pc, gate_c = _stream_in(xc, mod_c, norm_g[l], w_in[l])

        zs = jnp.zeros((bsz, DN_HEADS, DN_DK, DN_DV), jnp.float32)
        o_dn_c, s_f, s_b = _dn_bidir(*_dn_prep(pc, dn_conv[l], dn_a_log[l], dn_dt_bias[l]), zs, zs)
        o_dn_l, _, _ = _dn_bidir(*_dn_prep(pl, dn_conv[l], dn_a_log[l], dn_dt_bias[l]), s_f, s_b)

        zh = jnp.zeros((bsz, HG_HEADS, HG_DK, HG_DV), jnp.float32)
        o_hg_c, h_f, h_b = _hg_bidir(*_hg_prep(pc, lb_all[:, l]), zh, zh)
        o_hg_l, _, _ = _hg_bidir(*_hg_prep(pl, lb_all[:, l]), h_f, h_b)

        lam_init = 0.8 - 0.6 * math.exp(-0.3 * l)
        lp = da_lambda[l].astype(jnp.float32)
        lam = jnp.exp(jnp.sum(lp[0] * lp[1])) - jnp.exp(jnp.sum(lp[2] * lp[3])) + lam_init
        qc, kc, vc = _da_prep(pc)
        ql, kl, vl = _da_prep(pl)
        ql = _axial_rope(ql, ang_r, ang_c)
        kl = _axial_rope(kl, ang_r, ang_c)
        o_da_l = _diff_latent(ql, jnp.concatenate([kl, kc], axis=1),
                              jnp.concatenate([vl, vc], axis=1), lam)

        ys_l = _branches(pl, o_dn_l, o_hg_l, o_da_l, fn_w[l], fn_b[l], dn_norm[l], hg_norm[l],
                         da_norm[l], lam_init)
        new_xl = xl + gate_l * _merge(pl['gate'], ys_l, w_branch[l], w_out[l])
        if not last:
            o_da_c = _diff_block(qc, kc, vc, lam)
            ys_c = _branches(pc, o_dn_c, o_hg_c, o_da_c, fn_w[l], fn_b[l], dn_norm[l], hg_norm[l],
                             da_norm[l], lam_init)
            xc = xc + gate_c * _merge(pc['gate'], ys_c, w_branch[l], w_out[l])
        xl = new_xl
    return _rms(xl, final_g)
```

```python
import math
from contextlib import ExitStack, contextmanager
import numpy as np
import ml_dtypes
import concourse.bass as bass
import concourse.mybir as mybir
from concourse.bass_utils import run_bass_kernel_spmd

F32 = mybir.dt.float32
BF16 = mybir.dt.bfloat16
AF = mybir.ActivationFunctionType
ALU = mybir.AluOpType
AX = mybir.AxisListType

SKIP = set()
DNS = 99
D = 2048
KD = 16
INW = 15888
EPS = 1e-6


class Sem:
    __slots__ = ("h", "count", "name")

    def __init__(self, h, name):
        self.h = h
        self.count = 0
        self.name = name


class Buf:
    __slots__ = ("t", "w", "r", "name", "sem", "excl")

    def __init__(self, t, name=""):
        self.excl = False
        self.t = t
        self.w = None
        self.r = {}
        self.name = name
        self.sem = None

    def __getitem__(self, idx):
        return self.t[idx]


class Ctx:
    def __init__(self, nc, es):
        self.nc = nc
        self.root = es
        self.es = es
        self.engs = {"pe": nc.tensor, "act": nc.scalar, "dve": nc.vector,
                     "pool": nc.gpsimd, "sp": nc.sync}
        self.all_sems = []
        self.free_sems = []
        self.esem = {k: self._mk_sem("e_" + k) for k in ("pe", "act", "dve", "pool")}
        self.seen = {k: {} for k in self.engs}
        self.n_ins = 0
        self.n_wait = 0
        self.uid = 0

    def _mk_sem(self, name):
        s = Sem(self.root.enter_context(self.nc.semaphore(name)), name)
        self.all_sems.append(s)
        return s

    def get_sem(self):
        if self.free_sems:
            return self.free_sems.pop()
        self.uid += 1
        return self._mk_sem("d%d" % self.uid)

    @contextmanager
    def scope(self):
        old = self.es
        sems_before = None
        with ExitStack() as es:
            self.es = es
            self._scope_sems = getattr(self, "_scope_sems", [])
            mark = len(self._scope_sems)
            yield
            self.barrier()
            for s in self._scope_sems[mark:]:
                self.free_sems.append(s)
            del self._scope_sems[mark:]
        self.es = old

    def sbuf(self, name, shape, dt=F32, dma=False):
        self.uid += 1
        b = Buf(self.es.enter_context(self.nc.sbuf_tensor("%s_%d" % (name, self.uid), list(shape), dt)), name)
        if dma:
            b.sem = self.get_sem()
            if hasattr(self, "_scope_sems") and self.es is not self.root:
                self._scope_sems.append(b.sem)
        return b

    def psum(self, name, shape, dt=F32):
        self.uid += 1
        b = Buf(self.es.enter_context(self.nc.psum_tensor("%s_%d" % (name, self.uid), list(shape), dt)), name)
        b.excl = True
        return b

    def _wait(self, eng, sem, val):
        if self.seen[eng].get(sem, 0) >= val:
            return
        self.engs[eng].wait_ge(sem.h, val)
        self.seen[eng][sem] = val
        self.n_wait += 1

    def _deps(self, eng, reads, writes):
        need = {}
        own = self.esem.get(eng)
        for b in reads:
            if b.w is not None:
                s, v = b.w
                if v > need.get(s, 0):
                    need[s] = v
            if b.excl:
                for s, v in b.r.items():
                    if s is not own and v > need.get(s, 0):
                        need[s] = v
        for b in writes:
            if b.w is not None:
                s, v = b.w
                if v > need.get(s, 0):
                    need[s] = v
            for s, v in b.r.items():
                if s is own:
                    continue
                if v > need.get(s, 0):
                    need[s] = v
        for s, v in need.items():
            if s is own and eng == "pe":
                continue
            self._wait(eng, s, v)

    def _mark(self, tok, reads, writes):
        s, v = tok
        for b in reads:
            if v > b.r.get(s, 0):
                b.r[s] = v
        for b in writes:
            b.w = tok
            b.r = {}

    def op(self, eng, fn, reads=(), writes=()):
        self._deps(eng, reads, writes)
        ins = fn(self.engs[eng])
        s = self.esem[eng]
        s.count += 1
        ins.then_inc(s.h, 1)
        self._mark((s, s.count), reads, writes)
        self.n_ins += 1

    def mm(self, fns, reads=(), writes=()):
        self._deps("pe", reads, writes)
        ins = None
        for fn in fns:
            ins = fn(self.nc.tensor)
            self.n_ins += 1
        s = self.esem["pe"]
        s.count += 1
        ins.then_inc(s.h, 1)
        self._mark((s, s.count), reads, writes)

    def dma(self, q, sem, pairs, reads=(), writes=(), **kw):
        self._deps(q, reads, writes)
        for out_ap, in_ap in pairs:
            ins = self.engs[q].dma_start(out=out_ap, in_=in_ap, **kw)
            sem.count += 16
            ins.then_inc(sem.h, 16)
            self.n_ins += 1
        self._mark((sem, sem.count), reads, writes)

    def load(self, buf, dst, src, q="sp", **kw):
        self.dma(q, buf.sem, [(dst, src)], writes=[buf], **kw)

    def store(self, buf, dst, src, q="sp", **kw):
        self.dma(q, buf.sem, [(dst, src)], reads=[buf], **kw)

    def barrier(self):
        for e in self.engs:
            for s in self.all_sems:
                if s.count > 0:
                    self._wait(e, s, s.count)


class _SkipBody(Exception):
    pass


class _Skip:
    def __enter__(self):
        import sys
        sys.settrace(lambda *a, **k: None)
        frame = sys._getframe(1)
        frame.f_trace = self._trace

    def _trace(self, frame, event, arg):
        raise _SkipBody()

    def __exit__(self, t, v, tb):
        import sys
        sys.settrace(None)
        return t is _SkipBody


class Rot:
    def __init__(self, bufs):
        self.b = bufs
        self.i = 0

    def next(self):
        b = self.b[self.i % len(self.b)]
        self.i += 1
        return b


FM_GROUPS = [0, 512, 2560, 3088, 5136, 5648, 6160, 7184]
TM_GROUPS = [(1024, 512, 0), (1536, 512, 512), (2048, 512, 1024), (3600, 512, 1536), (4112, 512, 2048),
             (4624, 512, 2560), (6672, 512, 3072), (3072, 16, 3584)]
NTM = 3600
GATE0 = 7696


def build(NL, DEPTH, dbg=False):
    TL = NL * 128
    TC = 256
    NT = NL + 2
    NTOK = NT * 128
    nc = bass.Bass("TRN2", target_bir_lowering=False)

    def din(name, shape, dt=F32):
        return nc.dram_tensor(name, list(shape), dt, kind="ExternalInput").ap()

    def dscr(name, shape, dt=F32):
        kind = "ExternalOutput" if dbg else "Internal"
        return nc.dram_tensor(name, list(shape), dt, kind=kind).ap()

    x_in = din("x", [TL, D]); ctx_in = din("ctx", [TC, D])
    ccol = din("ccol", [128, KD, 2]); gcol = din("gcol", [128, DEPTH, KD])
    w_ada = din("w_ada", [DEPTH, D, 3 * D]); brow = din("brow", [2, DEPTH, 3 * D])
    w_in = din("w_in", [DEPTH, D, INW]); fn_w = din("fn_w", [DEPTH, 512, 512])
    fnb = din("fnb", [128, DEPTH, 4]); convw = din("convw", [128, DEPTH, 3, 1536])
    alog = din("alog", [128, DEPTH, 8]); dtb = din("dtb", [128, DEPTH, 8])
    hnorm = din("hnorm", [128, DEPTH, 3]); lblog = din("lblog", [128, 2, DEPTH, 512])
    dlam = din("dlam", [128, DEPTH, 4, 64]); fing = din("fing", [128, D])
    w_br = din("w_branch", [DEPTH, 4, 512, D]); w_out = din("w_out", [DEPTH, D, D])
    k_ident = din("k_ident", [128, 128]); k_ones = din("k_ones", [128, 128])
    k_onesb = din("k_onesb", [128, 128], BF16)
    k_hgm = din("k_hgm", [128, 2, 2, 4, 128]); k_dnm = din("k_dnm", [128, 2, 2, 4, 128])
    k_sub = din("k_sub", [128, 4]); k_sub2 = din("k_sub2", [128, 2])
    k_ropeR = din("k_ropeR", [128, 128]); k_cos = din("k_cos", [128, TL]); k_sin = din("k_sin", [128, TL])
    k_cs128 = din("k_cs128", [128, 256], BF16)
    k_ctl = din("k_ctl", [TL, TL], BF16); k_nstl = din("k_nstl", [TL, TL], BF16)
    k_ctc = din("k_ctc", [TC, TC], BF16); k_nstc = din("k_nstc", [TC, TC], BF16)
    out = nc.dram_tensor("out", [TL, D], F32, kind="ExternalOutput").ap()

    xs = dscr("xs", [NTOK, D])
    modrows = dscr("modrows", [DEPTH, 2, 3 * D])
    pfm = dscr("pfm", [32 * 128, NTOK])
    ptm = dscr("ptm", [NTOK, NTM])
    ysT = dscr("ysT", [16 * 128, NTOK], BF16)
    osc = dscr("osc", [2, 4 * 128, NTOK])

    def gtok(tile):
        return tile * 128

    def is_ctx(tile):
        return tile >= NL

    with ExitStack() as root:
        c = Ctx(nc, root)

        def A(fn, r=(), w=()):
            c.op("act", fn, r, w)

        def V(fn, r=(), w=()):
            c.op("dve", fn, r, w)

        def G(fn, r=(), w=()):
            c.op("pool", fn, r, w)

        ident = c.sbuf("ident", [128, 128], F32, dma=True)
        ones = c.sbuf("ones", [128, 128], F32, dma=True)
        onesb = c.sbuf("onesb", [128, 128], BF16, dma=True)
        epsc = c.sbuf("epsc", [128, 1], F32)
        c.load(ident, ident[:], k_ident)
        c.load(ones, ones[:], k_ones)
        c.load(onesb, onesb[:], k_onesb)
        V(lambda e: e.memset(epsc[:], EPS), [], [epsc])
        gcol_s = c.sbuf("gcol", [128, DEPTH, KD], F32, dma=True)
        c.load(gcol_s, gcol_s[:], gcol)
        hnorm_s = c.sbuf("hnorm", [128, DEPTH, 3], F32, dma=True)
        c.load(hnorm_s, hnorm_s[:], hnorm)
        fnb_s = c.sbuf("fnb", [128, DEPTH, 4], F32, dma=True)
        c.load(fnb_s, fnb_s[:], fnb)
        msc = c.sbuf("msc", [128, 2, 32], F32, dma=True)
        Gs = c.sbuf("Gs", [128, 2, KD], F32)
        gate_rep = c.sbuf("gate_rep", [128, 2, D], F32, dma=True)

        cp = c.get_sem()
        c.dma("sp", cp, [(xs[0:TL, :], x_in), (xs[TL:NTOK, :], ctx_in)])
        c.barrier()

        with c.scope():
            sc = c.sbuf("sc", [128, KD, 2], F32, dma=True)
            c.load(sc, sc[:], ccol)
            A(lambda e: e.activation(out=sc[:], in_=sc[:], func=AF.Silu), [sc], [sc])
            brow_s = c.sbuf("brow", [2, DEPTH, 3 * D], F32, dma=True)
            c.load(brow_s, brow_s[:], brow)
            wa = Rot([c.sbuf("wa", [128, KD, 512], F32, dma=True) for _ in range(2)])
            mrow = c.sbuf("mrow", [2, 3 * D], F32, dma=True)
            pp = Rot([c.psum("p0", [128, 512]) for _ in range(2)])
            for l in range(DEPTH):
                for g in range(12):
                    w = wa.next()
                    c.load(w, w[:], w_ada[l, :, g * 512:(g + 1) * 512].rearrange("(k p) n -> p k n", p=128))
                    ps = pp.next()
                    c.mm([(lambda e, k=k, w=w, ps=ps: e.matmul(ps[0:2, :], lhsT=sc[:, k, :], rhs=w[:, k, :],
                                                               start=(k == 0), stop=(k == KD - 1)))
                          for k in range(KD)], [sc, w], [ps])
                    V(lambda e, g=g, ps=ps, l=l: e.tensor_tensor(out=mrow[:, g * 512:(g + 1) * 512], in0=ps[0:2, :],
                                                                 in1=brow_s[:, l, g * 512:(g + 1) * 512], op=ALU.add),
                      [ps, brow_s], [mrow])
                c.store(mrow, modrows[l], mrow[:])

        for l in range(DEPTH):
            last = (l == DEPTH - 1)
            lam_init = 0.8 - 0.6 * math.exp(-0.3 * l)
            for j in range(2):
                c.dma("sp", msc.sem, [(msc[:, j, :], modrows[l, j, 0:2 * D].rearrange("(t p) -> p t", p=128))],
                      writes=[msc], allow_slow_non_contiguous=True)
                c.dma("sp", gate_rep.sem, [(gate_rep[:, j, :], modrows[l, j, 2 * D:3 * D].partition_broadcast(128))],
                      writes=[gate_rep])
            for j in range(2):
                V(lambda e, j=j: e.scalar_tensor_tensor(out=Gs[:, j, :], in0=msc[:, j, KD:2 * KD], scalar=1.0,
                                                        in1=gcol_s[:, l, :], op0=ALU.add, op1=ALU.mult),
                  [msc, gcol_s], [Gs])

            def build_hT(hT, tiles, xin_rot, xh, ss, rstd, ptr, evq):
                for i, gt in enumerate(tiles):
                    j = 1 if is_ctx(gt) else 0
                    xin = xin_rot.next()
                    c.load(xin, xin[:], xs[gtok(gt):gtok(gt) + 128, :])
                    A(lambda e, xin=xin: e.activation(out=xh[:], in_=xin[:], func=AF.Square, accum_out=ss[:]),
                      [xin], [xh, ss])
                    A(lambda e: e.activation(out=rstd[:], in_=ss[:], func=AF.Sqrt, bias=epsc[:], scale=1.0 / D),
                      [ss, epsc], [rstd])
                    V(lambda e: e.reciprocal(out=rstd[:], in_=rstd[:]), [rstd], [rstd])
                    V(lambda e, xin=xin: e.tensor_scalar(out=xh[:], in0=xin[:], scalar1=rstd[:, 0:1], scalar2=None,
                                                         op0=ALU.mult), [xin, rstd], [xh])
                    for kq in range(4):
                        pt = ptr.next()
                        c.mm([(lambda e, k=kq * 4 + kk, kk=kk, pt=pt: e.transpose(out=pt[:, kk * 128:(kk + 1) * 128],
                                                                                  in_=xh[:, k * 128:(k + 1) * 128],
                                                                                  identity=ident[:]))
                              for kk in range(4)], [xh, ident], [pt])
                        for kk in range(4):
                            k = kq * 4 + kk
                            fn = (lambda e, k=k, kk=kk, pt=pt, i=i, j=j: e.activation(
                                out=hT[:, k, i * 128:(i + 1) * 128], in_=pt[:, kk * 128:(kk + 1) * 128],
                                func=AF.Identity, bias=msc[:, j, k:k + 1], scale=Gs[:, j, k:k + 1]))
                            A(fn, [pt, msc, Gs], [hT])

            def blocks_of(tiles):
                res = []
                i = 0
                while i < len(tiles):
                    j = i + 1
                    while j < len(tiles) and j - i < 4 and tiles[j] == tiles[j - 1] + 1 and is_ctx(tiles[j]) == is_ctx(tiles[i]):
                        j += 1
                    res.append((i, j, gtok(tiles[i]), (j - i) * 128))
                    i = j
                return res

            if NL >= 32:
                sbs1 = [list(range(s * 16, s * 16 + 16)) + [NL + s] for s in range(2)]
            else:
                sbs1 = [list(range(NT))]
            with c.scope():
                maxt = max(len(s) for s in sbs1)
                hT = c.sbuf("hT", [128, KD, maxt * 128], BF16)
                wb = Rot([c.sbuf("wb", [128, KD, 512], BF16, dma=True) for _ in range(2)])
                xin_rot = Rot([c.sbuf("xin", [128, D], F32, dma=True) for _ in range(2)])
                xh = c.sbuf("xh", [128, D], F32)
                ss = c.sbuf("ss", [128, 1], F32)
                rstd = c.sbuf("rstd", [128, 1], F32)
                stg = Rot([c.sbuf("stg", [128, 512], F32, dma=True) for _ in range(4)])
                ptr = Rot([c.psum("ptr", [128, 512]) for _ in range(2)])
                pg = Rot([c.psum("pg", [128, 512]) for _ in range(4)])
                evi = [0]

                def evac(dst_fn, src_fn, reads, writes):
                    evi[0] += 1
                    if evi[0] % 2:
                        A(lambda e: e.copy(out=dst_fn(), in_=src_fn()), reads, writes)
                    else:
                        V(lambda e: e.tensor_copy(out=dst_fn(), in_=src_fn()), reads, writes)

                for tiles in sbs1:
                    build_hT(hT, tiles, xin_rot, xh, ss, rstd, ptr, None)
                    blks = blocks_of(tiles)
                    for gi, c0 in enumerate(FM_GROUPS):
                        w = wb.next()
                        c.load(w, w[:], w_in[l, :, c0:c0 + 512].rearrange("(k p) n -> p k n", p=128), q="pool")
                        for ct in range(4):
                            for (a, b, g0, ntok) in blks:
                                ps = pg.next()
                                c.mm([(lambda e, k=k, ps=ps, w=w, ct=ct, a=a, b=b, ntok=ntok: e.matmul(
                                    ps[:, 0:ntok], lhsT=w[:, k, ct * 128:(ct + 1) * 128], rhs=hT[:, k, a * 128:b * 128],
                                    start=(k == 0), stop=(k == KD - 1))) for k in range(KD)], [w, hT], [ps])
                                st = stg.next()
                                evac(lambda st=st, ntok=ntok: st[:, 0:ntok], lambda ps=ps, ntok=ntok: ps[:, 0:ntok], [ps], [st])
                                row = (gi * 4 + ct) * 128
                                c.store(st, pfm[row:row + 128, g0:g0 + ntok], st[:, 0:ntok])
                    for (c0, n, off) in TM_GROUPS:
                        w = wb.next()
                        c.load(w, w[:, :, 0:n], w_in[l, :, c0:c0 + n].rearrange("(k p) n -> p k n", p=128), q="pool")
                        for i, gt in enumerate(tiles):
                            ps = pg.next()
                            c.mm([(lambda e, k=k, ps=ps, w=w, i=i, n=n: e.matmul(
                                ps[:, 0:n], lhsT=hT[:, k, i * 128:(i + 1) * 128], rhs=w[:, k, 0:n],
                                start=(k == 0), stop=(k == KD - 1))) for k in range(KD)], [w, hT], [ps])
                            st = stg.next()
                            evac(lambda st=st, n=n: st[:, 0:n], lambda ps=ps, n=n: ps[:, 0:n], [ps], [st])
                            c.store(st, ptm[gtok(gt):gtok(gt) + 128, off:off + n], st[:, 0:n])

            if 'mix' not in SKIP:
                mixers(c, nc, dict(locals()))

            if NL >= 32:
                sbs3 = [list(range(s * 8, s * 8 + 8)) for s in range(NL // 8)] + ([[NL, NL + 1]] if not last else [])
            else:
                sbs3 = [list(range(NT if not last else NL))]
            with (c.scope() if 'p3' not in SKIP else _Skip()):
                maxt = max(len(s) for s in sbs3)
                hT = c.sbuf("hT3", [128, KD, maxt * 128], BF16)
                ysb = c.sbuf("ysb", [128, 16, maxt * 128], BF16, dma=True)
                yT = c.sbuf("yT", [128, KD, maxt * 128], BF16)
                acc = c.sbuf("acc", [128, 4, maxt * 128], F32)
                wg = Rot([c.sbuf("wg", [128, KD, 512], BF16, dma=True) for _ in range(2)])
                wbr = Rot([c.sbuf("wbr", [128, 4, 512], BF16, dma=True) for _ in range(2)])
                xin_rot = Rot([c.sbuf("xin3", [128, D], F32, dma=True) for _ in range(1)])
                xh = c.sbuf("xh3", [128, D], F32)
                ss = c.sbuf("ss3", [128, 1], F32)
                rstd = c.sbuf("rstd3", [128, 1], F32)
                sig = Rot([c.sbuf("sig", [128, 512], F32) for _ in range(2)])
                tmp = Rot([c.sbuf("tmp3", [128, 512], F32) for _ in range(2)])
                xo = Rot([c.sbuf("xo", [128, 512], F32, dma=True) for _ in range(3)])
                ptr = Rot([c.psum("ptr3", [128, 512]) for _ in range(2)])
                pga = Rot([c.psum("pga", [128, 512]) for _ in range(2)])
                pgb = Rot([c.psum("pgb", [128, 512]) for _ in range(2)])
                pgo = Rot([c.psum("pgo", [128, 512]) for _ in range(2)])
                for tiles in sbs3:
                    build_hT(hT, tiles, xin_rot, xh, ss, rstd, ptr, None)
                    blks = blocks_of(tiles)
                    for (a, b, g0, ntok) in blks:
                        c.dma("sp", ysb.sem, [(ysb[:, :, a * 128:b * 128],
                                               ysT[:, g0:g0 + ntok].rearrange("(t p) n -> p t n", p=128))], writes=[ysb])
                    for cg in range(4):
                        for n in range(4):
                            w = wg.next()
                            c0 = GATE0 + n * D + cg * 512
                            c.load(w, w[:], w_in[l, :, c0:c0 + 512].rearrange("(k p) n -> p k n", p=128), q="pool")
                            w2 = wbr.next()
                            c.load(w2, w2[:], w_br[l, n, :, cg * 512:(cg + 1) * 512].rearrange("(k p) n -> p k n", p=128), q="pool")
                            for ct in range(4):
                                for (a, b, g0, ntok) in blks:
                                    pa = pga.next()
                                    c.mm([(lambda e, k=k, pa=pa, w=w, ct=ct, a=a, b=b, ntok=ntok: e.matmul(
                                        pa[:, 0:ntok], lhsT=w[:, k, ct * 128:(ct + 1) * 128], rhs=hT[:, k, a * 128:b * 128],
                                        start=(k == 0), stop=(k == KD - 1))) for k in range(KD)], [w, hT], [pa])
                                    pb = pgb.next()
                                    c.mm([(lambda e, k=k, pb=pb, w2=w2, ct=ct, a=a, b=b, ntok=ntok, n=n: e.matmul(
                                        pb[:, 0:ntok], lhsT=w2[:, k, ct * 128:(ct + 1) * 128], rhs=ysb[:, n * 4 + k, a * 128:b * 128],
                                        start=(k == 0), stop=(k == 3))) for k in range(4)], [w2, ysb], [pb])
                                    sg = sig.next()
                                    A(lambda e, sg=sg, pa=pa, ntok=ntok: e.activation(out=sg[:, 0:ntok], in_=pa[:, 0:ntok],
                                                                                     func=AF.Sigmoid), [pa], [sg])
                                    if n == 0:
                                        V(lambda e, sg=sg, pb=pb, ct=ct, a=a, b=b, ntok=ntok: e.tensor_tensor(
                                            out=acc[:, ct, a * 128:b * 128], in0=pb[:, 0:ntok], in1=sg[:, 0:ntok], op=ALU.mult),
                                          [pb, sg], [acc])
                                    else:
                                        tm = tmp.next()
                                        V(lambda e, sg=sg, pb=pb, tm=tm, ntok=ntok: e.tensor_tensor(
                                            out=tm[:, 0:ntok], in0=pb[:, 0:ntok], in1=sg[:, 0:ntok], op=ALU.mult), [pb, sg], [tm])
                                        if n < 3:
                                            V(lambda e, tm=tm, ct=ct, a=a, b=b, ntok=ntok: e.tensor_tensor(
                                                out=acc[:, ct, a * 128:b * 128], in0=acc[:, ct, a * 128:b * 128],
                                                in1=tm[:, 0:ntok], op=ALU.add), [tm, acc], [acc])
                                        else:
                                            V(lambda e, tm=tm, ct=ct, a=a, b=b, ntok=ntok, cg=cg: e.tensor_tensor(
                                                out=yT[:, cg * 4 + ct, a * 128:b * 128], in0=acc[:, ct, a * 128:b * 128],
                                                in1=tm[:, 0:ntok], op=ALU.add), [tm, acc], [yT])
                    for cg in range(4):
                        w = wg.next()
                        c.load(w, w[:], w_out[l, :, cg * 512:(cg + 1) * 512].rearrange("(k p) n -> p k n", p=128), q="pool")
                        for i, gt in enumerate(tiles):
                            j = 1 if is_ctx(gt) else 0
                            xt = xo.next()
                            c.load(xt, xt[:], xs[gtok(gt):gtok(gt) + 128, cg * 512:(cg + 1) * 512])
                            po = pgo.next()
                            c.mm([(lambda e, k=k, po=po, w=w, i=i: e.matmul(
                                po[:], lhsT=yT[:, k, i * 128:(i + 1) * 128], rhs=w[:, k, :],
                                start=(k == 0), stop=(k == KD - 1))) for k in range(KD)], [w, yT], [po])
                            tm = tmp.next()
                            V(lambda e, po=po, tm=tm, j=j, cg=cg: e.tensor_tensor(
                                out=tm[:], in0=po[:], in1=gate_rep[:, j, cg * 512:(cg + 1) * 512], op=ALU.mult),
                              [po, gate_rep], [tm])
                            V(lambda e, tm=tm, xt=xt: e.tensor_tensor(out=xt[:], in0=xt[:], in1=tm[:], op=ALU.add),
                              [tm, xt], [xt])
                            c.store(xt, xs[gtok(gt):gtok(gt) + 128, cg * 512:(cg + 1) * 512], xt[:])

        with c.scope():
            fg = c.sbuf("fg", [128, D], F32, dma=True)
            c.load(fg, fg[:], fing)
            xin_rot = Rot([c.sbuf("xinf", [128, D], F32, dma=True) for _ in range(3)])
            xh = c.sbuf("xhf", [128, D], F32)
            ss = c.sbuf("ssf", [128, 1], F32)
            rstd = c.sbuf("rstdf", [128, 1], F32)
            for gt in range(NL):
                xin = xin_rot.next()
                c.load(xin, xin[:], xs[gtok(gt):gtok(gt) + 128, :])
                A(lambda e, xin=xin: e.activation(out=xh[:], in_=xin[:], func=AF.Square, accum_out=ss[:]), [xin], [xh, ss])
                A(lambda e: e.activation(out=rstd[:], in_=ss[:], func=AF.Sqrt, bias=epsc[:], scale=1.0 / D), [ss, epsc], [rstd])
                V(lambda e: e.reciprocal(out=rstd[:], in_=rstd[:]), [rstd], [rstd])
                V(lambda e, xin=xin: e.scalar_tensor_tensor(out=xin[:], in0=xin[:], scalar=rstd[:, 0:1], in1=fg[:],
                                                            op0=ALU.mult, op1=ALU.mult), [xin, rstd, fg], [xin])
                c.store(xin, out[gtok(gt):gtok(gt) + 128, :], xin[:])
        c.barrier()
        print("instructions", c.n_ins, "waits", c.n_wait, "sems", len(c.all_sems))
    return nc


def mixers(c, nc, E):
    NL, TL, TC, NT, NTOK = E["NL"], E["TL"], E["TC"], E["NT"], E["NTOK"]
    l, last, lam_init, DEPTH = E["l"], E["last"], E["lam_init"], E["DEPTH"]
    pfm, ptm, ysT, osc = E["pfm"], E["ptm"], E["ysT"], E["osc"]
    ident, ones, onesb, epsc = E["ident"], E["ones"], E["onesb"], E["epsc"]
    hnorm_s, fnb_s = E["hnorm_s"], E["fnb_s"]

    def op(eng, name, r, w, **kw):
        c.op(eng, lambda e: getattr(e, name)(**kw), r, w)

    def A(name, r, w, **kw):
        op("act", name, r, w, **kw)

    def V(name, r, w, **kw):
        op("dve", name, r, w, **kw)

    def G(name, r, w, **kw):
        op("pool", name, r, w, **kw)

    def ACT(r, w, out, in_, func, **kw):
        A("activation", r, w, out=out, in_=in_, func=func, **kw)

    def MM(items, r, w):
        c.mm([(lambda e, o=o, lt=lt, rh=rh, st=st, sp=sp: e.matmul(o, lhsT=lt, rhs=rh, start=st, stop=sp))
              for (o, lt, rh, st, sp) in items], r, w)

    def TR(items, r, w):
        c.mm([(lambda e, o=o, i=i: e.transpose(out=o, in_=i, identity=ident[:])) for (o, i) in items],
             list(r) + [ident], w)

    def fm_rows(tile0, n=4):
        return pfm[tile0 * 128:(tile0 + n) * 128, :].rearrange("(h p) n -> p h n", p=128)

    def ys_rows(br):
        return ysT[br * 512:(br + 1) * 512, :].rearrange("(h p) n -> p h n", p=128)

    def post(pb, o_sb, n, normcol, z_src, y_dst, ps_ss):
        zt, sqb, rs, yb = pb
        c.load(zt, z_src[0], z_src[1])
        ACT([o_sb], [sqb], sqb[:, 0:n], o_sb[:, 0:n], AF.Square)
        MM([(ps_ss[:, 0:n], ones[:], sqb[:, 0:n], True, True)], [ones, sqb], [ps_ss])
        ACT([ps_ss, epsc], [rs], rs[:, 0:n], ps_ss[:, 0:n], AF.Sqrt, bias=epsc[:], scale=1.0 / 128)
        V("reciprocal", [rs], [rs], out=rs[:, 0:n], in_=rs[:, 0:n])
        V("tensor_tensor", [rs, o_sb], [rs], out=rs[:, 0:n], in0=rs[:, 0:n], in1=o_sb[:, 0:n], op=ALU.mult)
        ACT([zt], [zt], zt[:, 0:n], zt[:, 0:n], AF.Silu)
        V("scalar_tensor_tensor", [rs, zt], [yb], out=yb[:, 0:n], in0=rs[:, 0:n], scalar=normcol, in1=zt[:, 0:n],
          op0=ALU.mult, op1=ALU.mult)
        c.store(yb, y_dst[0], y_dst[1])

    def post_bufs():
        return (c.sbuf("zt", [128, 512], F32, dma=True), c.sbuf("sqb", [128, 512], F32),
                c.sbuf("rs", [128, 512], F32), c.sbuf("yb", [128, 512], BF16, dma=True))

    def v3(buf, n=4):
        return buf[:].rearrange("p (h n) -> p h n", h=n)

    def fnet(tiles, ct_d, nst_d, T):
        nt = len(tiles)
        g0 = tiles[0] * 128
        scale = 1.0 / math.sqrt(T * 128.0)
        with c.scope():
            cs = c.sbuf("cs", [128, 256], BF16, dma=True)
            c.load(cs, cs[:], E["k_cs128"])
            AB = c.sbuf("AB", [128, nt, 4, 256], BF16)
            fw = c.sbuf("fw", [128, 4, 512], BF16, dma=True)
            c.load(fw, fw[:], E["fn_w"][l].rearrange("(g p) n -> p g n", p=128), q="pool")
            ur = Rot([c.sbuf("u", [128, 4, 128], BF16, dma=True) for _ in range(2)])
            ps1 = Rot([c.psum("ps1", [128, 512]) for _ in range(2)])
            for t, gt in enumerate(tiles):
                u = ur.next()
                c.load(u, u[:], fm_rows(0)[:, :, gt * 128:(gt + 1) * 128], q="pool")
                for half in range(2):
                    ps = ps1.next()
                    MM([(ps[:, gg * 256:(gg + 1) * 256], u[:, half * 2 + gg, :], cs[:], True, True) for gg in range(2)],
                       [u, cs], [ps])
                    dst = AB[:, t, half * 2:half * 2 + 2, :]
                    if half == 0:
                        A("copy", [ps], [AB], out=dst, in_=ps[:].rearrange("p (g n) -> p g n", g=2))
                    else:
                        V("tensor_copy", [ps], [AB], out=dst, in_=ps[:].rearrange("p (g n) -> p g n", g=2))
            PB = 256
            ctr = Rot([c.sbuf("ctb", [128, nt, PB], BF16, dma=True) for _ in range(2)])
            nsr = Rot([c.sbuf("nsb", [128, nt, PB], BF16, dma=True) for _ in range(2)])
            ztr = Rot([c.sbuf("fz", [128, 4, PB], F32, dma=True) for _ in range(2)])
            fT = c.sbuf("fT", [128, 4, PB], BF16)
            ybr = Rot([c.sbuf("fy", [128, 4, PB], BF16, dma=True) for _ in range(2)])
            psf = Rot([c.psum("psf", [128, 512]) for _ in range(2)])
            psy = Rot([c.psum("psy", [128, 512]) for _ in range(2)])
            for pb in range(T // PB):
                ctb = ctr.next(); nsb = nsr.next(); zt = ztr.next(); yb = ybr.next()
                for t0 in range(0, nt, 8):
                    t1_ = min(nt, t0 + 8)
                    c.load(ctb, ctb[:, t0:t1_, :], ct_d[t0 * 128:t1_ * 128, pb * PB:(pb + 1) * PB].rearrange("(t p) n -> p t n", p=128))
                    c.load(nsb, nsb[:, t0:t1_, :], nst_d[t0 * 128:t1_ * 128, pb * PB:(pb + 1) * PB].rearrange("(t p) n -> p t n", p=128))
                c.load(zt, zt[:], fm_rows(4)[:, :, g0 + pb * PB:g0 + (pb + 1) * PB])
                ACT([zt], [zt], zt[:], zt[:], AF.Silu)
                for g in range(4):
                    ps = psf.next()
                    items = []
                    for t in range(nt):
                        items.append((ps[:, 0:PB], AB[:, t, g, 0:128], ctb[:, t, :], t == 0, False))
                        items.append((ps[:, 0:PB], AB[:, t, g, 128:256], nsb[:, t, :], False, t == nt - 1))
                    MM(items, [AB, ctb, nsb], [ps])
                    ACT([ps], [fT], fT[:, g, :], ps[:, 0:PB], AF.Copy, scale=scale)
                for oc in range(4):
                    ps = psy.next()
                    MM([(ps[:, 0:PB], fw[:, g, oc * 128:(oc + 1) * 128], fT[:, g, :], g == 0, g == 3) for g in range(4)],
                       [fw, fT], [ps])
                    V("scalar_tensor_tensor", [ps, fnb_s, zt], [yb], out=yb[:, oc, :], in0=ps[:, 0:PB],
                      scalar=fnb_s[:, l, oc:oc + 1], in1=zt[:, oc, :], op0=ALU.add, op1=ALU.mult)
                c.store(yb, ys_rows(0)[:, :, g0 + pb * PB:g0 + (pb + 1) * PB], yb[:])

    if 'fn' not in SKIP:
        fnet(list(range(NL)), E["k_ctl"], E["k_nstl"], TL)
        fnet([NL, NL + 1], E["k_ctc"], E["k_nstc"], TC)

    with (c.scope() if 'da' not in SKIP else _Skip()):
        QB = min(512, TL)
        qT = c.sbuf("qT", [128, 4, TL], BF16)
        kT = c.sbuf("kT", [128, 4, NTOK], BF16, dma=True)
        qTc = c.sbuf("qTc", [128, 4, TC], BF16, dma=True)
        vb = c.sbuf("vb", [128, NT, 512], BF16, dma=True)
        for t0 in range(0, NT, 8):
            t1_ = min(NT, t0 + 8)
            c.load(vb, vb[:, t0:t1_, :], ptm[t0 * 128:t1_ * 128, 3072:3584].rearrange("(t p) n -> p t n", p=128), q="pool")
        ropeR = c.sbuf("ropeR", [128, 128], F32, dma=True)
        c.load(ropeR, ropeR[:], E["k_ropeR"])
        dl = c.sbuf("dl", [128, 4, 64], F32, dma=True)
        c.load(dl, dl[:], E["dlam"][:, l])
        lw = c.sbuf("lw", [128, 2, 64], F32)
        ls = c.sbuf("ls", [128, 2], F32)
        nlam = c.sbuf("nlam", [128, 1], F32)
        dnc = c.sbuf("dnc", [128, 1], F32)
        V("tensor_tensor", [dl], [lw], out=lw[:, 0, :], in0=dl[:, 0, :], in1=dl[:, 1, :], op=ALU.mult)
        V("tensor_tensor", [dl], [lw], out=lw[:, 1, :], in0=dl[:, 2, :], in1=dl[:, 3, :], op=ALU.mult)
        V("tensor_reduce", [lw], [ls], out=ls[:], in_=lw[:], axis=AX.X, op=ALU.add)
        ACT([ls], [ls], ls[:], ls[:], AF.Exp)
        V("tensor_tensor", [ls], [nlam], out=nlam[:], in0=ls[:, 1:2], in1=ls[:, 0:1], op=ALU.subtract)
        V("tensor_scalar", [nlam], [nlam], out=nlam[:], in0=nlam[:], scalar1=-lam_init, scalar2=None, op0=ALU.add)
        V("tensor_scalar", [hnorm_s], [dnc], out=dnc[:], in0=hnorm_s[:, l, 2:3], scalar1=1.0 - lam_init, scalar2=None,
          op0=ALU.mult)
        c.load(kT, kT[:, :, TL:NTOK], fm_rows(24)[:, :, TL:NTOK], q="pool")
        c.load(qTc, qTc[:], fm_rows(20)[:, :, TL:NTOK], q="pool")
        rawr = Rot([c.sbuf("raw", [128, 512], F32, dma=True) for _ in range(2)])
        cosb = c.sbuf("cosb", [128, 512], F32, dma=True)
        sinb = c.sbuf("sinb", [128, 512], F32, dma=True)
        t1r = Rot([c.sbuf("t1", [128, 512], F32) for _ in range(2)])
        t2r = Rot([c.sbuf("t2", [128, 512], F32) for _ in range(2)])
        pmisc = c.psum("pmisc", [128, 512])
        for blk in range(TL // QB):
            sl = slice(blk * QB, (blk + 1) * QB)
            c.load(cosb, cosb[:, 0:QB], E["k_cos"][:, sl])
            c.load(sinb, sinb[:, 0:QB], E["k_sin"][:, sl])
            for (base, dst) in ((20, qT), (24, kT)):
                for h in range(4):
                    raw = rawr.next(); t1 = t1r.next(); t2 = t2r.next()
                    c.load(raw, raw[:, 0:QB], pfm[(base + h) * 128:(base + h + 1) * 128, sl])
                    MM([(pmisc[:, 0:QB], ropeR[:], raw[:, 0:QB], True, True)], [ropeR, raw], [pmisc])
                    G("tensor_tensor", [raw, cosb], [t1], out=t1[:, 0:QB], in0=raw[:, 0:QB], in1=cosb[:, 0:QB], op=ALU.mult)
                    V("tensor_tensor", [pmisc, sinb], [t2], out=t2[:, 0:QB], in0=pmisc[:, 0:QB], in1=sinb[:, 0:QB], op=ALU.mult)
                    V("tensor_tensor", [t1, t2], [dst], out=dst[:, h, sl], in0=t1[:, 0:QB], in1=t2[:, 0:QB], op=ALU.add)
        ps_s = Rot([c.psum("ps_s", [128, 512]) for _ in range(3)])
        po = [c.psum("po", [128, 512]) for _ in range(2)]
        pz = [c.psum("pz", [128, 512]) for _ in range(2)]
        Er = Rot([c.sbuf("E", [128, 512], BF16) for _ in range(4)])
        rz = [c.sbuf("rz", [128, 512], F32) for _ in range(2)]
        fa = c.sbuf("fa", [128, 512], F32)
        fb = c.sbuf("fb", [128, 512], F32)
        osb = c.sbuf("osb", [128, 512], F32)
        pbufs = post_bufs()

        def attn(qbuf, qsl, nq, key_tiles, tok0, h):
            for m in range(2):
                pend = None
                nk = len(key_tiles)

                def oz(p):
                    Eb, kt, idx = p
                    MM([(po[m][:, 0:nq], vb[:, kt, h * 128:(h + 1) * 128], Eb[:, 0:nq], idx == 0, idx == nk - 1),
                        (pz[m][:, 0:nq], onesb[:], Eb[:, 0:nq], idx == 0, idx == nk - 1)],
                       [vb, onesb, Eb], [po[m], pz[m]])
                for idx, kt in enumerate(key_tiles):
                    ps = ps_s.next()
                    MM([(ps[:, 0:nq], kT[m * 64:(m + 1) * 64, h, kt * 128:(kt + 1) * 128],
                         qbuf[m * 64:(m + 1) * 64, h, qsl], True, True)], [kT, qbuf], [ps])
                    if pend is not None:
                        oz(pend)
                    Eb = Er.next()
                    ACT([ps], [Eb], Eb[:, 0:nq], ps[:, 0:nq], AF.Exp, scale=0.125)
                    pend = (Eb, kt, idx)
                oz(pend)
            for m in range(2):
                V("reciprocal", [pz[m]], [rz[m]], out=rz[m][:, 0:nq], in_=pz[m][:, 0:nq])
            V("tensor_tensor", [po[0], rz[0]], [fa], out=fa[:, 0:nq], in0=po[0][:, 0:nq], in1=rz[0][:, 0:nq], op=ALU.mult)
            V("tensor_tensor", [po[1], rz[1]], [fb], out=fb[:, 0:nq], in0=po[1][:, 0:nq], in1=rz[1][:, 0:nq], op=ALU.mult)
            V("scalar_tensor_tensor", [fb, nlam, fa], [osb], out=osb[:, 0:nq], in0=fb[:, 0:nq], scalar=nlam[:, 0:1],
              in1=fa[:, 0:nq], op0=ALU.mult, op1=ALU.add)
            zt = pbufs[0]; yb = pbufs[3]
            post(pbufs, osb, nq, dnc[:, 0:1],
                 (zt[:, 0:nq], pfm[(28 + h) * 128:(29 + h) * 128, tok0:tok0 + nq]),
                 (ysT[(12 + h) * 128:(13 + h) * 128, tok0:tok0 + nq], yb[:, 0:nq]), pmisc)

        for h in range(4):
            for blk in range(TL // QB):
                attn(qT, slice(blk * QB, (blk + 1) * QB), QB, list(range(NT)), blk * QB, h)
            if not last:
                attn(qTc, slice(0, TC), TC, [NL, NL + 1], TL, h)

    def order(dr):
        if dr == 0:
            return [NL, NL + 1] + list(range(NL))
        return [NL + 1, NL] + list(range(NL - 1, -1, -1))

    with (c.scope() if 'hg' not in SKIP else _Skip()):
        hgm = c.sbuf("hgm", [128, 2, 2, 4, 128], F32, dma=True)
        c.load(hgm, hgm[:], E["k_hgm"])
        sub = c.sbuf("sub", [128, 4], F32, dma=True)
        c.load(sub, sub[:], E["k_sub"])
        lb = c.sbuf("lb", [128, 2, 512], F32)
        omlb = c.sbuf("omlb", [128, 2, 512], F32)
        with c.scope():
            lg = c.sbuf("lg", [128, 2, DEPTH, 512], F32, dma=True)
            c.load(lg, lg[:], E["lblog"])
            tot = c.sbuf("tot", [128, 2, 512], F32)
            ACT([lg], [lg], lg[:], lg[:], AF.Exp)
            V("tensor_copy", [lg], [tot], out=tot[:], in_=lg[:, :, 0, :])
            for ll in range(1, DEPTH):
                V("tensor_tensor", [tot, lg], [tot], out=tot[:], in0=tot[:], in1=lg[:, :, ll, :], op=ALU.add)
            V("reciprocal", [tot], [tot], out=tot[:], in_=tot[:])
            V("memset", [], [lb], lb[:], 0.0) if False else c.op("dve", lambda e: e.memset(lb[:], 0.0), [], [lb])
            for ll in range(1, l + 1):
                V("tensor_tensor", [lb, lg], [lb], out=lb[:], in0=lb[:], in1=lg[:, :, ll, :], op=ALU.add)
            V("tensor_tensor", [lb, tot], [lb], out=lb[:], in0=lb[:], in1=tot[:], op=ALU.mult)
            V("tensor_scalar", [lb], [omlb], out=omlb[:], in0=lb[:], scalar1=-1.0, scalar2=1.0, op0=ALU.mult, op1=ALU.add)
        S = [c.sbuf("S", [128, 128], F32) for _ in range(4)]
        Sb = [c.sbuf("Sb", [128, 128], BF16) for _ in range(4)]
        frr = Rot([c.sbuf("fr", [128, 512], F32, dma=True) for _ in range(2)])
        vtr = Rot([c.sbuf("vt", [128, 512], BF16, dma=True) for _ in range(2)])
        qtr = Rot([c.sbuf("qt", [128, 512], F32, dma=True) for _ in range(2)])
        logf = c.sbuf("logf", [128, 512], F32)
        kk = c.sbuf("kk", [128, 512], F32)
        eH = c.sbuf("eH", [128, 512], F32)
        khm = c.sbuf("khm", [128, 4, 512], BF16)
        eG = c.sbuf("eG", [128, 512], F32)
        enG = c.sbuf("enG", [128, 512], F32)
        ktil = c.sbuf("ktil", [128, 512], BF16)
        qtil = c.sbuf("qtil", [128, 512], BF16)
        ATm = c.sbuf("ATm", [128, 512], BF16)
        osb = c.sbuf("osbh", [128, 512], F32, dma=True)
        ofr = c.sbuf("ofr", [128, 512], F32, dma=True)
        pbufs = post_bufs()
        pH = c.psum("pH", [128, 512]); pGT = c.psum("pGT", [128, 512]); pKT = c.psum("pKT", [128, 512])
        pAT = c.psum("pAT", [128, 512]); pO = c.psum("pO", [128, 512])
        pSr = Rot([c.psum("pS", [128, 512]) for _ in range(2)])
        pss = c.psum("pss", [128, 512])
        for dr in range(2):
            for h in range(4):
                c.op("dve", lambda e, h=h: e.memset(S[h][:], 0.0), [], [S[h]])
                c.op("dve", lambda e, h=h: e.memset(Sb[h][:], 0.0), [], [Sb[h]])
            if dr == 1:
                c.barrier()
            Mincl = hgm[:, dr, 0, 0, :]
            Mrest = hgm[:, dr, 1, 0, :]
            for gt in order(dr):
                tok = gt * 128
                fr = frr.next(); vt = vtr.next(); qt = qtr.next()
                c.load(fr, fr[:], ptm[tok:tok + 128, 1536 + dr * 512:2048 + dr * 512])
                c.load(vt, vt[:], ptm[tok:tok + 128, 2560:3072], q="pool")
                c.load(qt, v3(qt), fm_rows(12)[:, :, tok:tok + 128])
                ACT([qt], [qt], qt[:], qt[:], AF.Silu)
                ACT([fr], [fr], fr[:], fr[:], AF.Sigmoid)
                V("tensor_tensor", [fr, omlb], [fr], out=fr[:], in0=fr[:], in1=omlb[:, dr, :], op=ALU.mult)
                V("tensor_tensor", [fr, lb], [fr], out=fr[:], in0=fr[:], in1=lb[:, dr, :], op=ALU.add)
                ACT([fr], [logf], logf[:], fr[:], AF.Ln)
                V("tensor_scalar", [fr], [kk], out=kk[:], in0=fr[:], scalar1=-1.0, scalar2=1.0, op0=ALU.mult, op1=ALU.add)
                MM([(pH[:], Mrest, logf[:], True, True)], [hgm, logf], [pH])
                MM([(pGT[:, h * 128:(h + 1) * 128], logf[:, h * 128:(h + 1) * 128], Mincl, True, True) for h in range(4)],
                   [hgm, logf], [pGT])
                TR([(pKT[:, h * 128:(h + 1) * 128], kk[:, h * 128:(h + 1) * 128]) for h in range(4)], [kk], [pKT])
                ACT([pH], [eH], eH[:], pH[:], AF.Exp)
                ACT([pGT], [eG], eG[:], pGT[:], AF.Exp)
                ACT([pGT], [enG], enG[:], pGT[:], AF.Exp, scale=-1.0)
                G("tensor_tensor", [kk, eH], [eH], out=eH[:], in0=kk[:], in1=eH[:], op=ALU.mult)
                for a in range(4):
                    G("tensor_scalar", [eH, sub], [khm], out=khm[:, a, :], in0=eH[:], scalar1=sub[:, a:a + 1], scalar2=None,
                      op0=ALU.mult)
                V("tensor_tensor", [pKT, enG], [ktil], out=ktil[:], in0=pKT[:], in1=enG[:], op=ALU.mult)
                V("tensor_tensor", [qt, eG], [qtil], out=qtil[:], in0=qt[:], in1=eG[:], op=ALU.mult)
                MM([(pAT[:, h * 128:(h + 1) * 128], ktil[:, h * 128:(h + 1) * 128], qtil[:, h * 128:(h + 1) * 128], True, True)
                    for h in range(4)], [ktil, qtil], [pAT])
                V("tensor_tensor", [pAT, hgm], [ATm], out=v3(ATm), in0=v3(pAT), in1=hgm[:, dr, 0, :, :], op=ALU.mult)
                MM([(pO[:, h * 128:(h + 1) * 128], vt[:, h * 128:(h + 1) * 128], ATm[:, h * 128:(h + 1) * 128], h == 0, False)
                    for h in range(4)], [vt, ATm], [pO])
                subs = range(4) if dr == 0 else range(3, -1, -1)
                for si, a in enumerate(subs):
                    MM([(pO[:, h * 128 + a * 32:h * 128 + (a + 1) * 32], Sb[h][:], qtil[:, h * 128 + a * 32:h * 128 + (a + 1) * 32],
                         False, si == 3) for h in range(4)], Sb + [qtil], [pO])
                    pS = pSr.next()
                    MM([(pS[:, h * 128:(h + 1) * 128], khm[:, a, h * 128:(h + 1) * 128], vt[:, h * 128:(h + 1) * 128], True, True)
                        for h in range(4)], [khm, vt], [pS])
                    li = a * 32 + (31 if dr == 0 else 0)
                    for h in range(4):
                        V("scalar_tensor_tensor", [S[h], eG, pS], [S[h]], out=S[h][:], in0=S[h][:],
                          scalar=eG[:, h * 128 + li:h * 128 + li + 1], in1=pS[:, h * 128:(h + 1) * 128],
                          op0=ALU.mult, op1=ALU.add)
                        A("copy", [S[h]], [Sb[h]], out=Sb[h][:], in_=S[h][:])
                dst = osc[1].rearrange("(h p) n -> p h n", p=128)[:, :, tok:tok + 128]
                if dr == 0:
                    A("copy", [pO], [osb], out=osb[:], in_=pO[:])
                    c.store(osb, dst, v3(osb))
                else:
                    c.load(ofr, v3(ofr), dst)
                    V("tensor_tensor", [pO, ofr], [osb], out=osb[:], in0=pO[:], in1=ofr[:], op=ALU.add)
                    zt = pbufs[0]; yb = pbufs[3]
                    post(pbufs, osb, 512, hnorm_s[:, l, 1:2], (v3(zt), fm_rows(16)[:, :, tok:tok + 128]),
                         (ys_rows(2)[:, :, tok:tok + 128], v3(yb)), pss)

    with (c.scope() if 'dn' not in SKIP else _Skip()):
        dnm = c.sbuf("dnm", [128, 2, 2, 4, 128], F32, dma=True)
        c.load(dnm, dnm[:], E["k_dnm"])
        cw = c.sbuf("cw", [128, 3, 1536], F32, dma=True)
        c.load(cw, cw[:], E["convw"][:, l])
        nA = c.sbuf("nA", [128, 8], F32, dma=True)
        c.load(nA, nA[:], E["alog"][:, l])
        ACT([nA], [nA], nA[:], nA[:], AF.Exp)
        V("tensor_scalar", [nA], [nA], out=nA[:], in0=nA[:], scalar1=-1.0, scalar2=None, op0=ALU.mult)
        dtb_s = c.sbuf("dtb", [128, 8], F32, dma=True)
        c.load(dtb_s, dtb_s[:], E["dtb"][:, l])
        onec = c.sbuf("onec", [128, 1], F32)
        c.op("dve", lambda e: e.memset(onec[:], 1.0), [], [onec])
        S = [c.sbuf("S", [128, 128], F32) for _ in range(4)]
        Sb = [c.sbuf("Sb", [128, 128], BF16) for _ in range(4)]
        x0r = Rot([c.sbuf("x0", [128, 1536], F32, dma=True) for _ in range(2)])
        xmr = Rot([c.sbuf("xm", [128, 1536], F32, dma=True) for _ in range(2)])
        xpr = Rot([c.sbuf("xp", [128, 1536], F32, dma=True) for _ in range(2)])
        abr = Rot([c.sbuf("ab", [128, 16], F32, dma=True) for _ in range(2)])
        tA = c.sbuf("tA", [128, 1536], F32)
        tB = c.sbuf("tB", [128, 1536], F32)
        qkv = c.sbuf("qkv", [128, 1536], F32)
        ssq = c.sbuf("ssq", [128, 8], F32)
        qkn = c.sbuf("qkn", [128, 1024], F32)
        g4 = c.sbuf("g4", [128, 4], F32)
        beta = c.sbuf("beta", [128, 4], F32)
        gcs = c.sbuf("gcs", [128, 8], F32)
        egs = c.sbuf("egs", [128, 8], F32)
        bg = c.sbuf("bg", [128, 4], F32)
        gbh = c.sbuf("gbh", [128, 512], F32)
        dtmp = c.sbuf("dtmp", [128, 512], F32)
        Dm = c.sbuf("Dm", [128, 512], F32)
        DTm = c.sbuf("DTm", [128, 512], F32)
        egc = c.sbuf("egc", [128, 512], F32)
        knT = c.sbuf("knT", [128, 512], F32)
        knTb = c.sbuf("knTb", [128, 512], BF16)
        qnTb = c.sbuf("qnTb", [128, 512], BF16)
        qdT = c.sbuf("qdT", [128, 512], BF16)
        Pa = [c.sbuf("Pa", [128, 512], F32) for _ in range(2)]
        Pt = [c.sbuf("Pt", [128, 512], F32) for _ in range(2)]
        XT = c.sbuf("XT", [128, 512], F32)
        vbt = c.sbuf("vbt", [128, 512], F32)
        kbg = c.sbuf("kbg", [128, 512], F32)
        U = c.sbuf("U", [128, 512], F32)
        wT = c.sbuf("wT", [128, 512], BF16)
        aqkT = c.sbuf("aqkT", [128, 512], BF16)
        kd = c.sbuf("kd", [128, 512], BF16)
        vnew = c.sbuf("vnew", [128, 512], BF16)
        kdm = c.sbuf("kdm", [128, 2, 512], BF16)
        sub2 = c.sbuf("sub2", [128, 2], F32, dma=True)
        c.load(sub2, sub2[:], E["k_sub2"])
        osb = c.sbuf("osbd", [128, 512], F32, dma=True)
        ofr = c.sbuf("ofrd", [128, 512], F32, dma=True)
        pbufs = post_bufs()
        B = [c.psum("pb%d" % i, [128, 512]) for i in range(8)]
        for dr in range(2):
            for h in range(4):
                c.op("dve", lambda e, h=h: e.memset(S[h][:], 0.0), [], [S[h]])
                c.op("dve", lambda e, h=h: e.memset(Sb[h][:], 0.0), [], [Sb[h]])
            if dr == 1:
                c.barrier()
            Mincl = dnm[:, dr, 0, 0, :]
            Mrest = dnm[:, dr, 1, 0, :]
            for gt in order(dr):
                tok = gt * 128
                s0, s1 = (0, TL) if gt < NL else (TL, NTOK)
                x0 = x0r.next(); xm = xmr.next(); xp = xpr.next(); ab = abr.next()
                c.load(x0, x0[:], ptm[tok:tok + 128, 0:1536])
                if tok - 1 >= s0:
                    c.load(xm, xm[:], ptm[tok - 1:tok + 127, 0:1536])
                else:
                    c.op("dve", lambda e, xm=xm: e.memset(xm[:], 0.0), [], [xm])
                    c.load(xm, xm[1:128, :], ptm[tok:tok + 127, 0:1536])
                if tok + 129 <= s1:
                    c.load(xp, xp[:], ptm[tok + 1:tok + 129, 0:1536])
                else:
                    c.op("dve", lambda e, xp=xp: e.memset(xp[:], 0.0), [], [xp])
                    c.load(xp, xp[0:127, :], ptm[tok + 1:tok + 128, 0:1536])
                c.load(ab, ab[:], ptm[tok:tok + 128, 3584:3600])
                if DNS < 1:
                    continue
                V("tensor_tensor", [xm, cw], [tA], out=tA[:], in0=xm[:], in1=cw[:, 0, :], op=ALU.mult)
                G("tensor_tensor", [x0, cw], [tB], out=tB[:], in0=x0[:], in1=cw[:, 1, :], op=ALU.mult)
                V("tensor_tensor", [tA, tB], [tA], out=tA[:], in0=tA[:], in1=tB[:], op=ALU.add)
                G("tensor_tensor", [xp, cw], [tB], out=tB[:], in0=xp[:], in1=cw[:, 2, :], op=ALU.mult)
                V("tensor_tensor", [tA, tB], [tA], out=tA[:], in0=tA[:], in1=tB[:], op=ALU.add)
                ACT([tA], [qkv], qkv[:], tA[:], AF.Silu)
                ACT([qkv], [tB], tB[:, 0:1024], qkv[:, 0:1024], AF.Square)
                V("tensor_reduce", [tB], [ssq], out=ssq[:], in_=tB[:, 0:1024].rearrange("p (h n) -> p h n", h=8),
                  axis=AX.X, op=ALU.add)
                ACT([ssq, epsc], [ssq], ssq[:], ssq[:], AF.Sqrt, bias=epsc[:], scale=1.0)
                V("reciprocal", [ssq], [ssq], out=ssq[:], in_=ssq[:])
                V("tensor_scalar", [ssq], [ssq], out=ssq[:, 0:4], in0=ssq[:, 0:4], scalar1=128.0 ** -0.5, scalar2=None,
                  op0=ALU.mult)
                for hh in range(8):
                    eng = V if hh % 2 == 0 else G
                    eng("tensor_scalar", [qkv, ssq], [qkn], out=qkn[:, hh * 128:(hh + 1) * 128],
                        in0=qkv[:, hh * 128:(hh + 1) * 128], scalar1=ssq[:, hh:hh + 1], scalar2=None, op0=ALU.mult)
                if DNS < 2:
                    continue
                V("tensor_tensor", [ab, dtb_s], [g4], out=g4[:], in0=ab[:, dr * 4:dr * 4 + 4], in1=dtb_s[:, dr * 4:dr * 4 + 4],
                  op=ALU.add)
                ACT([g4], [g4], g4[:], g4[:], AF.Exp)
                ACT([g4, onec], [g4], g4[:], g4[:], AF.Ln, bias=onec[:], scale=1.0)
                V("tensor_tensor", [g4, nA], [g4], out=g4[:], in0=g4[:], in1=nA[:, dr * 4:dr * 4 + 4], op=ALU.mult)
                ACT([ab], [beta], beta[:], ab[:, 8 + dr * 4:12 + dr * 4], AF.Sigmoid)
                for h in range(4):
                    V("tensor_scalar", [ones, g4], [gbh], out=gbh[:, h * 128:(h + 1) * 128], in0=ones[:], scalar1=g4[:, h:h + 1],
                      scalar2=None, op0=ALU.mult)
                MM([(B[0][:], Mincl, gbh[:], True, True)], [dnm, gbh], [B[0]])
                V("tensor_copy", [B[0]], [gcs], out=gcs[:, 0:4], in_=v3(B[0])[:, :, 0])
                MM([(B[0][:], Mrest, gbh[:], True, True)], [dnm, gbh], [B[0]])
                V("tensor_copy", [B[0]], [gcs], out=gcs[:, 4:8], in_=v3(B[0])[:, :, 0])
                ACT([gcs], [egs], egs[:], gcs[:], AF.Exp)
                if DNS < 3:
                    continue
                MM([(B[1][:, h * 128:(h + 1) * 128], gbh[:, h * 128:(h + 1) * 128], Mincl, True, True) for h in range(4)],
                   [gbh, dnm], [B[1]])
                if DNS < 3.2:
                    continue
                for h in range(4):
                    V("tensor_scalar", [B[1], gcs], [dtmp], out=dtmp[:, h * 128:(h + 1) * 128], in0=B[1][:, h * 128:(h + 1) * 128],
                      scalar1=gcs[:, h:h + 1], scalar2=None, op0=ALU.subtract)
                if DNS < 3.4:
                    continue
                V("tensor_scalar", [dtmp], [Dm], out=Dm[:], in0=dtmp[:], scalar1=0.0, scalar2=None, op0=ALU.max)
                ACT([Dm], [Dm], Dm[:], Dm[:], AF.Exp, scale=-1.0)
                V("tensor_tensor", [Dm, dnm], [Dm], out=v3(Dm), in0=v3(Dm), in1=dnm[:, dr, 1, :, :], op=ALU.mult)
                if DNS < 3.6:
                    continue
                V("tensor_scalar", [dtmp], [DTm], out=DTm[:], in0=dtmp[:], scalar1=0.0, scalar2=None, op0=ALU.min)
                ACT([DTm], [DTm], DTm[:], DTm[:], AF.Exp)
                V("tensor_tensor", [DTm, dnm], [DTm], out=v3(DTm), in0=v3(DTm), in1=dnm[:, dr, 0, :, :], op=ALU.mult)
                if DNS < 3.8:
                    continue
                ACT([B[1]], [egc], egc[:], B[1][:], AF.Exp)
                if DNS < 4:
                    continue
                TR([(B[2][:, h * 128:(h + 1) * 128], qkn[:, (4 + h) * 128:(5 + h) * 128]) for h in range(4)], [qkn], [B[2]])
                TR([(B[3][:, h * 128:(h + 1) * 128], qkn[:, h * 128:(h + 1) * 128]) for h in range(4)], [qkn], [B[3]])
                A("copy", [B[2]], [knT], out=knT[:], in_=B[2][:])
                V("tensor_copy", [B[2]], [knTb], out=knTb[:], in_=B[2][:])
                A("copy", [B[3]], [qnTb], out=qnTb[:], in_=B[3][:])
                V("tensor_tensor", [B[3], egc], [qdT], out=qdT[:], in0=B[3][:], in1=egc[:], op=ALU.mult)
                MM([(B[4][:, h * 128:(h + 1) * 128], knT[:, h * 128:(h + 1) * 128], knT[:, h * 128:(h + 1) * 128], True, True)
                    for h in range(4)], [knT], [B[4]])
                P, PT = Pa[0], Pt[0]
                for h in range(4):
                    V("scalar_tensor_tensor", [B[4], beta, Dm], [P], out=P[:, h * 128:(h + 1) * 128],
                      in0=B[4][:, h * 128:(h + 1) * 128], scalar=beta[:, h:h + 1], in1=Dm[:, h * 128:(h + 1) * 128],
                      op0=ALU.mult, op1=ALU.mult)
                TR([(B[5][:, h * 128:(h + 1) * 128], P[:, h * 128:(h + 1) * 128]) for h in range(4)], [P], [B[5]])
                A("copy", [B[5]], [PT], out=PT[:], in_=B[5][:])
                for h in range(4):
                    G("tensor_tensor", [PT, ident], [XT], out=XT[:, h * 128:(h + 1) * 128], in0=ident[:],
                      in1=PT[:, h * 128:(h + 1) * 128], op=ALU.subtract)
                if DNS < 5:
                    continue
                for lev in range(5):
                    Pn, PTn = Pa[(lev + 1) % 2], Pt[(lev + 1) % 2]
                    MM([(B[2][:, h * 128:(h + 1) * 128], PT[:, h * 128:(h + 1) * 128], P[:, h * 128:(h + 1) * 128], True, True)
                        for h in range(4)], [P, PT], [B[2]])
                    if lev < 4:
                        MM([(B[3][:, h * 128:(h + 1) * 128], P[:, h * 128:(h + 1) * 128], PT[:, h * 128:(h + 1) * 128], True, True)
                            for h in range(4)], [P, PT], [B[3]])
                    A("copy", [B[2]], [Pn], out=Pn[:], in_=B[2][:])
                    if lev < 4:
                        V("tensor_copy", [B[3]], [PTn], out=PTn[:], in_=B[3][:])
                    MM([(B[4][:, h * 128:(h + 1) * 128], Pn[:, h * 128:(h + 1) * 128], XT[:, h * 128:(h + 1) * 128], True, True)
                        for h in range(4)], [Pn, XT], [B[4]])
                    V("tensor_tensor", [B[4], XT], [XT], out=XT[:], in0=XT[:], in1=B[4][:], op=ALU.add)
                    P, PT = Pn, PTn
                if DNS < 6:
                    continue
                V("tensor_tensor", [beta, egs], [bg], out=bg[:], in0=beta[:], in1=egs[:, 0:4], op=ALU.mult)
                for h in range(4):
                    V("tensor_scalar", [qkv, beta], [vbt], out=vbt[:, h * 128:(h + 1) * 128], in0=qkv[:, 1024 + h * 128:1152 + h * 128],
                      scalar1=beta[:, h:h + 1], scalar2=None, op0=ALU.mult)
                    G("tensor_scalar", [qkn, bg], [kbg], out=kbg[:, h * 128:(h + 1) * 128], in0=qkn[:, (4 + h) * 128:(5 + h) * 128],
                      scalar1=bg[:, h:h + 1], scalar2=None, op0=ALU.mult)
                    G("tensor_scalar", [qkn, egs], [kd], out=kd[:, h * 128:(h + 1) * 128], in0=qkn[:, (4 + h) * 128:(5 + h) * 128],
                      scalar1=egs[:, 4 + h:5 + h], scalar2=None, op0=ALU.mult)
                MM([(B[5][:, h * 128:(h + 1) * 128], XT[:, h * 128:(h + 1) * 128], vbt[:, h * 128:(h + 1) * 128], True, True)
                    for h in range(4)], [XT, vbt], [B[5]])
                MM([(B[6][:, h * 128:(h + 1) * 128], kbg[:, h * 128:(h + 1) * 128], XT[:, h * 128:(h + 1) * 128], True, True)
                    for h in range(4)], [XT, kbg], [B[6]])
                A("copy", [B[5]], [U], out=U[:], in_=B[5][:])
                V("tensor_copy", [B[6]], [wT], out=wT[:], in_=B[6][:])
                MM([(B[7][:, h * 128:(h + 1) * 128], knTb[:, h * 128:(h + 1) * 128], qnTb[:, h * 128:(h + 1) * 128], True, True)
                    for h in range(4)], [knTb, qnTb], [B[7]])
                V("tensor_tensor", [B[7], DTm], [aqkT], out=aqkT[:], in0=B[7][:], in1=DTm[:], op=ALU.mult)
                if DNS < 7:
                    continue
                chunks = (0, 1) if dr == 0 else (1, 0)
                for cc in range(2):
                    G("tensor_scalar", [kd, sub2], [kdm], out=kdm[:, cc, :], in0=kd[:], scalar1=sub2[:, cc:cc + 1], scalar2=None,
                      op0=ALU.mult)
                for ci, cc in enumerate(chunks):
                    r0 = cc * 64
                    MM([(B[0][:, h * 128:(h + 1) * 128], wT[:, h * 128:(h + 1) * 128], Sb[h][:], True, True)
                        for h in range(4)], Sb + [wT], [B[0]])
                    V("tensor_tensor", [U, B[0]], [vnew], out=vnew[:], in0=U[:], in1=B[0][:], op=ALU.subtract)
                    items = []
                    for h in range(4):
                        o = B[1][:, h * 128 + r0:h * 128 + r0 + 64]
                        items.append((o, Sb[h][:], qdT[:, h * 128 + r0:h * 128 + r0 + 64], ci == 0 and h == 0, False))
                        items.append((o, vnew[:, h * 128:(h + 1) * 128], aqkT[:, h * 128 + r0:h * 128 + r0 + 64], False,
                                      ci == 1 and h == 3))
                    MM(items, Sb + [qdT, vnew, aqkT], [B[1]])
                    MM([(B[6][:, h * 128:(h + 1) * 128], kdm[:, cc, h * 128:(h + 1) * 128], vnew[:, h * 128:(h + 1) * 128], True, True)
                        for h in range(4)], [kdm, vnew], [B[6]])
                    li = r0 + (63 if dr == 0 else 0)
                    for h in range(4):
                        V("scalar_tensor_tensor", [S[h], egc, B[6]], [S[h]], out=S[h][:], in0=S[h][:],
                          scalar=egc[:, h * 128 + li:h * 128 + li + 1], in1=B[6][:, h * 128:(h + 1) * 128],
                          op0=ALU.mult, op1=ALU.add)
                        A("copy", [S[h]], [Sb[h]], out=Sb[h][:], in_=S[h][:])
                if DNS < 8:
                    continue
                dst = osc[0].rearrange("(h p) n -> p h n", p=128)[:, :, tok:tok + 128]
                if dr == 0:
                    A("copy", [B[1]], [osb], out=osb[:], in_=B[1][:])
                    c.store(osb, dst, v3(osb))
                else:
                    c.load(ofr, v3(ofr), dst)
                    V("tensor_tensor", [B[1], ofr], [osb], out=osb[:], in0=B[1][:], in1=ofr[:], op=ALU.add)
                    zt = pbufs[0]; yb = pbufs[3]
                    post(pbufs, osb, 512, hnorm_s[:, l, 0:1], (v3(zt), fm_rows(8)[:, :, tok:tok + 128]),
                         (ys_rows(1)[:, :, tok:tok + 128], v3(yb)), B[7])


def _bf(a):
    return np.ascontiguousarray(a).astype(ml_dtypes.bfloat16)


def _blk_masks(blk):
    p = np.arange(128)[:, None]
    f = np.arange(128)[None, :]
    same = (p // blk) == (f // blk)
    m = np.zeros((128, 2, 2, 4, 128), np.float32)
    m[:, 0, 0] = (same & (p <= f))[:, None, :]
    m[:, 0, 1] = (same & (p > f))[:, None, :]
    m[:, 1, 0] = (same & (p >= f))[:, None, :]
    m[:, 1, 1] = (same & (p < f))[:, None, :]
    return m


_CONST_CACHE = {}


def _consts(NL):
    if NL in _CONST_CACHE:
        return _CONST_CACHE[NL]
    TL = NL * 128
    k = {}
    k["k_ident"] = np.eye(128, dtype=np.float32)
    k["k_ones"] = np.ones((128, 128), np.float32)
    k["k_onesb"] = _bf(np.ones((128, 128), np.float32))
    k["k_hgm"] = _blk_masks(32)
    k["k_dnm"] = _blk_masks(64)
    k["k_sub"] = (np.arange(128)[:, None] // 32 == np.arange(4)[None, :]).astype(np.float32)
    k["k_sub2"] = (np.arange(128)[:, None] // 64 == np.arange(2)[None, :]).astype(np.float32)
    R = np.zeros((128, 128), np.float32)
    for pp in range(128):
        d = pp % 32
        if d < 16:
            R[pp, pp + 16] = -1.0
        else:
            R[pp, pp - 16] = 1.0
    k["k_ropeR"] = np.ascontiguousarray(R.T)
    pos = np.arange(TL)
    row = (pos // 64).astype(np.float32)
    col = (pos % 64).astype(np.float32)
    inv = (np.float32(10000.0) ** (-np.arange(16, dtype=np.float32) / np.float32(16))).astype(np.float32)
    ang = np.zeros((128, TL), np.float32)
    for pp in range(128):
        d = pp % 64
        i = d % 16
        ang[pp] = (row if d < 32 else col) * inv[i]
    k["k_cos"] = np.cos(ang).astype(np.float32)
    k["k_sin"] = np.sin(ang).astype(np.float32)
    cf = np.outer(np.arange(128), np.arange(128)).astype(np.float64) * (2 * np.pi / 128)
    k["k_cs128"] = _bf(np.concatenate([np.cos(cf), np.sin(cf)], axis=1))

    def dft(T):
        idx = (np.outer(np.arange(T), np.arange(T)) % T).astype(np.float64) * (2 * np.pi / T)
        return _bf(np.cos(idx)), _bf(-np.sin(idx))
    k["k_ctl"], k["k_nstl"] = dft(TL)
    k["k_ctc"], k["k_nstc"] = dft(256)
    _CONST_CACHE[NL] = k
    return k


def _rep(a, n=128):
    return np.ascontiguousarray(np.broadcast_to(a[None], (n,) + a.shape)).astype(np.float32)


def _col(a):
    a = np.asarray(a, np.float32)
    lead = a.shape[:-1]
    r = a.reshape(lead + (a.shape[-1] // 128, 128))
    return np.ascontiguousarray(np.moveaxis(r, -1, 0))


def _core_inputs(inp, b, NL, DEPTH):
    f = lambda a: np.ascontiguousarray(np.asarray(a, np.float32))
    m = dict(_consts(NL))
    m["x"] = f(inp["x"][b])
    m["ctx"] = f(inp["ctx"][b])
    cc = np.stack([_col(inp["c"][b]), _col(inp["c_ctx"])], axis=-1)
    m["ccol"] = f(cc)
    m["gcol"] = f(_col(inp["norm_g"]))
    m["w_ada"] = f(inp["w_ada"])
    m["brow"] = _rep(f(inp["b_ada"]), 2)
    m["w_in"] = f(inp["w_in"])
    m["fn_w"] = f(inp["fn_w"])
    m["fnb"] = f(_col(inp["fn_b"]))
    m["convw"] = _rep(f(inp["dn_conv"]))
    m["alog"] = _rep(f(inp["dn_a_log"]).reshape(DEPTH, 8))
    m["dtb"] = _rep(f(inp["dn_dt_bias"]).reshape(DEPTH, 8))
    hn = np.stack([f(inp["dn_norm"]), f(inp["hg_norm"]), f(inp["da_norm"])], axis=-1)
    m["hnorm"] = f(np.transpose(hn, (1, 0, 2)))
    m["lblog"] = _rep(f(inp["hg_lb_logits"]))
    m["dlam"] = _rep(f(inp["da_lambda"]))
    m["fing"] = _rep(f(inp["final_g"]))
    m["w_branch"] = f(inp["w_branch"])
    m["w_out"] = f(inp["w_out"])
    return m


_NC_CACHE = {}


def kernel(**inputs):
    B, T, _ = inputs["x"].shape
    DEPTH = inputs["w_in"].shape[0]
    NL = T // 128
    key = (NL, DEPTH)
    if key not in _NC_CACHE:
        _NC_CACHE[key] = build(NL, DEPTH)
    nc = _NC_CACHE[key]
    in_maps = [_core_inputs(inputs, b, NL, DEPTH) for b in range(B)]
    res = run_bass_kernel_spmd(nc, in_maps, core_ids=list(range(B)))
    return np.stack([np.asarray(r["out"], np.float32) for r in res.results], axis=0)
```
